# Optimizing a Trainium2 kernel written in Bass

```python
import jax, jax.numpy as jnp
from jax import lax
import numpy as np

D_MODEL = 2048
BATCH = 2
SEQ = 8192
DEPTH = 4

CHUNK = 64
DN_HEAD_DIM = 128
DN_WIDTH = D_MODEL // 2
DN_HEADS = DN_WIDTH // DN_HEAD_DIM
DN_CONV = 4
SG_WIDTH = D_MODEL // 4
SG_GROUPS = 4
SG_GROUP_DIM = SG_WIDTH // SG_GROUPS
SG_BLOCK = 128
CV_WIDTH = D_MODEL // 4
CV_GROUPS = 4
CV_KERNEL = 31
D_MIX = DN_WIDTH + SG_WIDTH + CV_WIDTH
IN_WIDTHS = (DN_WIDTH, DN_WIDTH, DN_WIDTH, DN_WIDTH, DN_HEADS, DN_HEADS,
             SG_WIDTH, SG_WIDTH, SG_WIDTH, CV_WIDTH, CV_WIDTH, CV_WIDTH)
D_IN = sum(IN_WIDTHS)
EPS = 1e-6
LN_EPS = 1e-5

kernel_name = "hybrid_deltanet_gmlp_conformer_trunk"


def rmsnorm(x, g):
    xf = x.astype(jnp.float32)
    y = xf * lax.rsqrt(jnp.mean(xf * xf, axis=-1, keepdims=True) + EPS)
    return (y * g.astype(jnp.float32)).astype(x.dtype)


def group_layernorm(x, g, b, groups):
    xf = x.astype(jnp.float32).reshape(x.shape[:-1] + (groups, -1))
    mu = jnp.mean(xf, axis=-1, keepdims=True)
    var = jnp.mean(jnp.square(xf - mu), axis=-1, keepdims=True)
    y = ((xf - mu) * lax.rsqrt(var + LN_EPS)).reshape(x.shape)
    return (y * g.astype(jnp.float32) + b.astype(jnp.float32)).astype(x.dtype)


def l2norm(t):
    return t * lax.rsqrt(jnp.sum(t * t, axis=-1, keepdims=True) + EPS)


def causal_dwconv(x, w):
    k = w.shape[0]
    return lax.conv_general_dilated(x, w[:, None, :], window_strides=(1,), padding=[(k - 1, 0)],
                                    dimension_numbers=('NWC', 'WIO', 'NWC'),
                                    feature_group_count=x.shape[-1])


def gated_delta_rule(q, k, v, g, beta):
    bsz, t_len, n_h, dk = q.shape
    dv = v.shape[-1]
    n = t_len // CHUNK

    def to_chunks(t):
        t = t.reshape((bsz, n, CHUNK, n_h) + t.shape[3:])
        return jnp.moveaxis(t, 3, 1)

    q = to_chunks(q) * (dk ** -0.5)
    k, v = to_chunks(k), to_chunks(v)
    beta, g = to_chunks(beta), to_chunks(g)
    gc = jnp.cumsum(g, axis=-1)
    tri_incl = jnp.tril(jnp.ones((CHUNK, CHUNK), dtype=bool))
    tri_strict = jnp.tril(jnp.ones((CHUNK, CHUNK), dtype=bool), k=-1)
    diff = gc[..., :, None] - gc[..., None, :]
    decay = jnp.where(tri_incl, jnp.exp(jnp.where(tri_incl, diff, 0.0)), 0.0)
    k_beta = k * beta[..., None]
    kk = jnp.einsum('bhncd,bhnsd->bhncs', k_beta, k) * decay
    a_mat = jnp.eye(CHUNK, dtype=jnp.float32) + jnp.where(tri_strict, kk, 0.0)
    rhs = jnp.concatenate([v * beta[..., None], k_beta * jnp.exp(gc)[..., None]], axis=-1)
    sol = lax.linalg.triangular_solve(a_mat, rhs, left_side=True, lower=True, unit_diagonal=True)
    u, w = sol[..., :dv], sol[..., dv:]
    qk = jnp.einsum('bhncd,bhnsd->bhncs', q, k) * decay

    xs = tuple(jnp.moveaxis(t, 2, 0) for t in (q, k, u, w, qk, gc))

    def step(state, inp):
        q_i, k_i, u_i, w_i, qk_i, g_i = inp
        v_new = u_i - jnp.einsum('bhck,bhkv->bhcv', w_i, state)
        o = (jnp.einsum('bhck,bhkv->bhcv', q_i * jnp.exp(g_i)[..., None], state)
             + jnp.einsum('bhcs,bhsv->bhcv', qk_i, v_new))
        g_last = g_i[..., -1:]
        state = (state * jnp.exp(g_last)[..., None]
                 + jnp.einsum('bhck,bhcv->bhkv', k_i * jnp.exp(g_last - g_i)[..., None], v_new))
        return state, o

    s0 = jnp.zeros((bsz, n_h, dk, dv), jnp.float32)
    _, o = lax.scan(step, s0, xs)
    return jnp.transpose(o, (1, 0, 3, 2, 4)).reshape(bsz, t_len, n_h, dv)


def hybrid_layer(x, mod, norm_g, w_in, conv_qkv, a_log, dt_bias, dn_norm_g,
                 sg_ln_g, sg_ln_b, sg_w, sg_b, cv_w, cv_b, cv_ln_g, cv_ln_b, w_out):
    bsz, t_len, _ = x.shape
    shift, scale, gate = jnp.split(mod, 3, axis=-1)
    h = rmsnorm(x, norm_g) * (1 + scale[:, None, :]) + shift[:, None, :]
    p = h @ w_in
    split_points = [int(s) for s in np.cumsum(IN_WIDTHS)[:-1]]
    (q, k, v, z, b_dn, a_dn, u_sg, v_sg, gate_sg, a_cv, b_cv, gate_cv) = jnp.split(p, split_points, axis=-1)

    qkv = jax.nn.silu(causal_dwconv(jnp.concatenate([q, k, v], axis=-1), conv_qkv))
    q, k, v = jnp.split(qkv, 3, axis=-1)
    heads = lambda t: t.reshape(bsz, t_len, DN_HEADS, DN_HEAD_DIM).astype(jnp.float32)
    q, k, v = l2norm(heads(q)), l2norm(heads(k)), heads(v)
    beta = jax.nn.sigmoid(b_dn.astype(jnp.float32))
    g = -jnp.exp(a_log.astype(jnp.float32)) * jax.nn.softplus(a_dn.astype(jnp.float32) + dt_bias.astype(jnp.float32))
    o = rmsnorm(gated_delta_rule(q, k, v, g, beta), dn_norm_g)
    y_dn = o.reshape(bsz, t_len, DN_WIDTH).astype(x.dtype) * jax.nn.silu(z)

    u_sg = jax.nn.gelu(u_sg, approximate=False)
    v_sg = group_layernorm(jax.nn.gelu(v_sg, approximate=False), sg_ln_g, sg_ln_b, SG_GROUPS)
    v_blk = v_sg.reshape(bsz, t_len // SG_BLOCK, SG_BLOCK, SG_GROUPS, SG_GROUP_DIM)
    pos_chunk = jnp.arange(SG_BLOCK) // CHUNK
    block_causal = pos_chunk[:, None] >= pos_chunk[None, :]
    w_s = jnp.where(block_causal, sg_w, 0)
    mixed = jnp.einsum('gts,bnsgc->bntgc', w_s, v_blk) + sg_b.T[None, None, :, :, None]
    y_sg = u_sg * mixed.reshape(bsz, t_len, SG_WIDTH) * jax.nn.silu(gate_sg)

    glu = a_cv * jax.nn.sigmoid(b_cv)
    dw = causal_dwconv(glu, cv_w) + cv_b
    y_cv = jax.nn.silu(group_layernorm(dw, cv_ln_g, cv_ln_b, CV_GROUPS)) * jax.nn.silu(gate_cv)

    y = jnp.concatenate([y_dn, y_sg, y_cv], axis=-1) @ w_out
    return x + gate[:, None, :] * y


def setup_inputs(seed: int = 0) -> dict:
    key = jax.random.key(seed)
    ks = jax.random.split(key, 24)
    f32 = jnp.float32
    nrm = lambda k, shape, s: jax.random.normal(k, shape, f32) * s
    L, D = DEPTH, D_MODEL
    dt = jnp.exp(jax.random.uniform(ks[7], (L, DN_HEADS), f32, np.log(1e-3), np.log(1e-1)))
    return {
        'x': nrm(ks[0], (BATCH, SEQ, D), 1.0),
        'c': nrm(ks[1], (BATCH, D), 1.0),
        'norm_g': 1.0 + nrm(ks[2], (L, D), 0.02),
        'w_ada': nrm(ks[3], (L, D, 3 * D), 0.5 * D ** -0.5),
        'b_ada': nrm(ks[4], (L, 3 * D), 0.02),
        'w_in': nrm(ks[5], (L, D, D_IN), D ** -0.5),
        'conv_qkv': nrm(ks[6], (L, DN_CONV, 3 * DN_WIDTH), DN_CONV ** -0.5),
        'a_log': jnp.log(jax.random.uniform(ks[8], (L, DN_HEADS), f32, 1.0, 16.0)),
        'dt_bias': dt + jnp.log(-jnp.expm1(-dt)),
        'dn_norm_g': 1.0 + nrm(ks[9], (L, DN_HEAD_DIM), 0.02),
        'sg_ln_g': 1.0 + nrm(ks[10], (L, SG_WIDTH), 0.02),
        'sg_ln_b': nrm(ks[11], (L, SG_WIDTH), 0.02),
        'sg_w': nrm(ks[12], (L, SG_GROUPS, SG_BLOCK, SG_BLOCK), SG_BLOCK ** -0.5),
        'sg_b': 1.0 + nrm(ks[13], (L, SG_GROUPS, SG_BLOCK), 0.02),
        'cv_w': nrm(ks[14], (L, CV_KERNEL, CV_WIDTH), CV_KERNEL ** -0.5),
        'cv_b': nrm(ks[15], (L, CV_WIDTH), 0.02),
        'cv_ln_g': 1.0 + nrm(ks[16], (L, CV_WIDTH), 0.02),
        'cv_ln_b': nrm(ks[17], (L, CV_WIDTH), 0.02),
        'w_out': nrm(ks[18], (L, D_MIX, D), D_MIX ** -0.5),
        'final_g': 1.0 + nrm(ks[19], (D,), 0.02),
    }


def reference(x, c, norm_g, w_ada, b_ada, w_in, conv_qkv, a_log, dt_bias, dn_norm_g,
              sg_ln_g, sg_ln_b, sg_w, sg_b, cv_w, cv_b, cv_ln_g, cv_ln_b, w_out, final_g):
    c_act = jax.nn.silu(c)
    for l in range(DEPTH):
        mod = c_act @ w_ada[l] + b_ada[l]
        x = hybrid_layer(x, mod, norm_g[l], w_in[l], conv_qkv[l], a_log[l], dt_bias[l], dn_norm_g[l],
                         sg_ln_g[l], sg_ln_b[l], sg_w[l], sg_b[l], cv_w[l], cv_b[l], cv_ln_g[l], cv_ln_b[l],
                         w_out[l])
    return rmsnorm(x, final_g)
```

```python
import numpy as np
from contextlib import ExitStack, contextmanager
import concourse.bass as bass
import concourse.mybir as mybir
from concourse.bass_utils import run_bass_kernel_spmd

F32 = mybir.dt.float32
BF16 = mybir.dt.bfloat16
I32 = mybir.dt.int32
AF = mybir.ActivationFunctionType
ALU = mybir.AluOpType

DEPTH = 4
D = 2048
KD = 16
SEQ = 8192
NCORE = 8
NTOK = 2048
HALO = 32
NTP = 1024
NPASS = 2
WP = HALO + NTP
TPP = NTP // 128
NH = 8
DIN = 7184
C_Q, C_K, C_V, C_Z, C_B, C_A = 0, 1024, 2048, 3072, 4096, 4104
C_USG, C_VSG, C_GSG, C_ACV, C_BCV, C_GCV = 4112, 4624, 5136, 5648, 6160, 6672
EPS = 1e-6
LN_EPS = 1e-5
MAGIC = 0x5f3759df
LNTEST = False
ENGS = ("tensor", "vector", "scalar", "gpsimd", "sync")


class Buf:
    __slots__ = ("name", "w", "r", "excl")

    def __init__(self, name=""):
        self.name = name
        self.w = None
        self.r = {}
        self.excl = False


class Tl:
    def __init__(self, t, name):
        self.t = t
        self.b = Buf(name)

    def __getitem__(self, k):
        return self.t[k]


class _Rec:
    def __getattr__(self, name):
        def f(*a, **kw):
            self.call = (name, a, kw)
            return self
        return f


class Prog:
    def __init__(self, nc, stack):
        self.nc = nc
        self.stack = stack
        self.stacks = [stack]
        self.q = {e: [] for e in ENGS}
        self.cnt = {}
        self.sems = {}
        self.known = {e: {} for e in ENGS}
        for e in ENGS:
            self._mksem(e)
        self.uid = 0
        self.nops = 0
        self.ndma = 0
        self.ncc = 0
        self.rs_tmp = {}

    def _mksem(self, key):
        self.sems[key] = self.stack.enter_context(self.nc.semaphore("s_" + str(key)))
        self.cnt[key] = 0

    def sb(self, name, shape, dt=F32):
        self.uid += 1
        nm = "%s_%d" % (name, self.uid)
        return Tl(self.stacks[-1].enter_context(self.nc.sbuf_tensor(nm, list(shape), dt)), nm)

    def ps(self, name, shape, dt=F32):
        self.uid += 1
        nm = "%s_%d" % (name, self.uid)
        tl = Tl(self.stack.enter_context(self.nc.psum_tensor(nm, list(shape), dt)), nm)
        tl.b.excl = True
        return tl

    @contextmanager
    def scope(self):
        st = ExitStack()
        self.stacks.append(st)
        try:
            yield
        finally:
            self.barrier()
            self.stacks.pop()
            st.close()

    def barrier(self):
        for e in ENGS:
            kn = self.known[e]
            waits = []
            for k, v in self.cnt.items():
                if k == e or v == 0:
                    continue
                if kn.get(k, 0) < v:
                    kn[k] = v
                    waits.append((k, v))
            if waits:
                self.q[e].append((waits, None, None))

    def _waits(self, eng, reads, writes):
        need = {}
        for b in reads:
            if b.w is not None:
                k, v = b.w
                if need.get(k, 0) < v:
                    need[k] = v
        for b in writes:
            if b.w is not None:
                k, v = b.w
                if need.get(k, 0) < v:
                    need[k] = v
            for k, v in b.r.items():
                if need.get(k, 0) < v:
                    need[k] = v
        out = []
        kn = self.known[eng]
        for k, v in need.items():
            if k == eng and eng == "tensor":
                continue
            if kn.get(k, 0) < v:
                kn[k] = v
                out.append((k, v))
        return out

    @staticmethod
    def _bufs(xs):
        return [x if isinstance(x, Buf) else x.b for x in xs]

    def op(self, eng, fn, reads=(), writes=(), inc=True):
        reads = self._bufs(reads)
        writes = self._bufs(writes)
        ex = [b for b in reads if b.excl]
        if ex:
            reads = [b for b in reads if not b.excl]
            writes = writes + ex
        rec = _Rec()
        fn(rec)
        call = rec.call
        waits = self._waits(eng, reads, writes)
        if inc:
            self.cnt[eng] += 1
            v = self.cnt[eng]
        else:
            v = self.cnt[eng] + 1
        self.q[eng].append((waits, call, (eng, 1) if inc else None))
        for b in reads:
            if b.r.get(eng, 0) < v:
                b.r[eng] = v
        for b in writes:
            b.w = (eng, v)
            b.r = {}
        self.nops += 1

    NDSEM = 32

    def dma(self, out_ap, in_ap, reads=(), writes=(), eng="sync"):
        reads = self._bufs(reads)
        writes = self._bufs(writes)
        semkey = "d%d" % (self.ndma % self.NDSEM)
        self.ndma += 1
        if semkey not in self.sems:
            self._mksem(semkey)
        waits = self._waits(eng, reads, writes)
        prev = self.cnt[semkey]
        if prev > 0 and self.known[eng].get(semkey, 0) < prev:
            self.known[eng][semkey] = prev
            waits.append((semkey, prev))
        self.cnt[semkey] += 16
        v = self.cnt[semkey]
        self.q[eng].append((waits, ("dma_start", (), dict(out=out_ap, in_=in_ap)), (semkey, 16)))
        for b in reads:
            if b.r.get(semkey, 0) < v:
                b.r[semkey] = v
        for b in writes:
            b.w = (semkey, v)
            b.r = {}
        self.nops += 1

    def coll(self, fn, reads=(), writes=()):
        reads = self._bufs(reads)
        writes = self._bufs(writes)
        semkey = "cc%d" % self.ncc
        self.ncc += 1
        self._mksem(semkey)
        rec = _Rec()
        fn(rec)
        waits = self._waits("gpsimd", reads, writes)
        self.cnt[semkey] = 1
        self.q["gpsimd"].append((waits, rec.call, (semkey, None)))
        for b in reads:
            b.r[semkey] = 1
        for b in writes:
            b.w = (semkey, 1)
            b.r = {}

    def emit(self):
        nc = self.nc
        self.barrier()
        with nc.Block() as block:
            def mk(ename):
                items = self.q[ename]

                def body(e):
                    for waits, call, inc in items:
                        for k, v in waits:
                            e.wait_ge(self.sems[k], v)
                        if call is None:
                            continue
                        name, a, kw = call
                        ins = getattr(e, name)(*a, **kw)
                        if inc is not None:
                            if inc[1] is None:
                                ins.then_inc(self.sems[inc[0]])
                            else:
                                ins.then_inc(self.sems[inc[0]], inc[1])
                return body
            block.tensor(mk("tensor"))
            block.vector(mk("vector"))
            block.scalar(mk("scalar"))
            block.gpsimd(mk("gpsimd"))
            block.sync(mk("sync"))


def host_consts():
    s = np.arange(128)[:, None]
    c = np.arange(128)[None, :]
    cst = {}
    cst["ident"] = np.eye(128, dtype=np.float32)
    cst["ones"] = np.ones((128, 128), np.float32)
    cst["maskneg"] = np.where(s <= c, 0.0, -1e30).astype(np.float32)
    cst["strict01"] = (s < c).astype(np.float32)
    cst["triu"] = (s <= c).astype(np.float32)
    lm = np.zeros((128, 7, 128), np.float32)
    for b in range(7):
        hb = 1 << b
        lm[:, b, :] = ((s // (2 * hb)) == (c // (2 * hb))) & ((s % (2 * hb)) < hb) & ((c % (2 * hb)) >= hb)
    cst["lvlmask"] = lm
    cst["wsmask"] = ((c // 64) >= (s // 64)).astype(np.float32)
    return cst


class _Stop(Exception):
    pass


def build(nlayers=DEPTH, debug=False, wl=DEPTH, stop=None, nocc=False):
    nc = bass.Bass("TRN2", target_bir_lowering=False)

    def din(name, shape, dt=F32):
        return nc.dram_tensor(name, list(shape), dt, kind="ExternalInput").ap()

    x_in = din("x_in", [NTOK, D])
    xh_in = din("xh_in", [HALO, D])
    c_in = din("c_in", [D])
    w_ada = din("w_ada", [wl, D, 1536])
    b_ada = din("b_ada", [wl, 1536])
    w_in = din("w_in", [wl, D, DIN])
    w_out = din("w_out", [wl, D, D])
    norm_g = din("norm_g", [DEPTH, D])
    conv_qkv = din("conv_qkv", [DEPTH, 4, 3072])
    a_log = din("a_log", [DEPTH, 8])
    dt_bias = din("dt_bias", [DEPTH, 8])
    dn_norm_g = din("dn_norm_g", [DEPTH, 128])
    sg_ln_g = din("sg_ln_g", [DEPTH, 512])
    sg_ln_b = din("sg_ln_b", [DEPTH, 512])
    sg_w = din("sg_w", [DEPTH, 4, 128, 128])
    sg_b = din("sg_b", [DEPTH, 4, 128])
    cv_w = din("cv_w", [DEPTH, 31, 512])
    cv_b = din("cv_b", [DEPTH, 512])
    cv_ln_g = din("cv_ln_g", [DEPTH, 512])
    cv_ln_b = din("cv_ln_b", [DEPTH, 512])
    final_g = din("final_g", [D])
    flags_in = din("flags", [128, 12])
    k_ident = din("k_ident", [128, 128])
    k_ones = din("k_ones", [128, 128])
    k_maskneg = din("k_maskneg", [128, 128])
    k_strict01 = din("k_strict01", [128, 128])
    k_triu = din("k_triu", [128, 128])
    k_lvlmask = din("k_lvlmask", [128, 7, 128])
    k_wsmask = din("k_wsmask", [128, 128])
    out_d = nc.dram_tensor("out", [NTOK, D], F32, kind="ExternalOutput").ap()
    dbg_d = None
    if debug:
        dbg_d = nc.dram_tensor("dbg", [128, 4096], F32, kind="ExternalOutput").ap()

    xT_d = nc.dram_tensor("xT_scr", [D, HALO + NTOK], F32)
    olocal_d = nc.dram_tensor("olocal_scr", [16, NH, 128, 128], F32)
    qtT_d = nc.dram_tensor("qtT_scr", [16, NH, 128, 128], BF16)
    szT_d = nc.dram_tensor("szT_scr", [NH, 128, NTOK], BF16)
    yscT_d = nc.dram_tensor("yscT_scr", [8, 128, NTOK], BF16)
    st_snd = nc.dram_tensor("st_snd", [NH * 128, 256], F32)
    st_rcv = nc.dram_tensor("st_rcv", [4 * NH * 128, 256], F32)
    xh_snd = nc.dram_tensor("xh_snd", [D, HALO], F32)
    xh_rcv = nc.dram_tensor("xh_rcv", [4 * D, HALO], F32)
    mod_snd = nc.dram_tensor("mod_snd", [DEPTH, 1536], F32)
    mod_rcv = nc.dram_tensor("mod_rcv", [4 * DEPTH, 1536], F32)
    RG = [[0, 1, 2, 3], [4, 5, 6, 7]]

    with ExitStack() as st:
        P = Prog(nc, st)
        def allgather(snd, rcv, bs, br):
            if nocc:
                n = snd.ap().shape[0]
                for r_ in range(4):
                    P.dma(rcv[r_ * n:(r_ + 1) * n, :], snd.ap(), reads=[bs], writes=[br])
            else:
                P.coll(lambda e: e.collective_compute("AllGather", ALU.bypass, replica_groups=RG,
                                                     ins=[snd.ap().opt()], outs=[rcv.ap().opt()]), [bs], [br])

        def ck(n):
            if stop == n:
                raise _Stop()
        b_xT = [Buf("xT%d" % k) for k in range(KD)]
        b_xTh = Buf("xT_halo")
        b_olocal = Buf("olocal")
        b_qtT = Buf("qtT")
        b_szT = Buf("szT")
        b_yscT = Buf("yscT")
        b_stsnd, b_strcv = Buf("stsnd"), Buf("strcv")
        b_xhsnd, b_xhrcv = Buf("xhsnd"), Buf("xhrcv")
        b_modsnd, b_modrcv = Buf("modsnd"), Buf("modrcv")
        b_out = Buf("out")
        b_dbg = Buf("dbg")

        try:
            PA = [P.ps("PA%d" % i, [128, 512]) for i in range(3)]
            PC = P.ps("PC", [128, 512])
            PT = P.ps("PT", [128, 1024], BF16)
            PS_ = P.ps("PS", [128, 512])
            PR = P.ps("PR", [128, 512])
            PQ = P.ps("PQ", [128, 512])
            class Reg:
                def __init__(self, tl, lo, hi, name):
                    self.tl = tl
                    self.lo, self.hi = lo, hi
                    self.b = tl.b

                def ap(self, n=None):
                    hi = self.hi if n is None else self.lo + n
                    return self.tl.t[:, self.lo:hi]
            R_PS = [Reg(PS_, i * 128, (i + 1) * 128, "PS%d" % i) for i in range(4)]
            R_PA2 = [Reg(PA[2], i * 128, (i + 1) * 128, "PA2_%d" % i) for i in range(4)]
            R_ks = Reg(PR, 0, 256, "ks")
            R_xr = Reg(PR, 256, 512, "xr")
            R_o = Reg(PQ, 0, 128, "o")
            R_qt = Reg(PQ, 128, 256, "qt")
            R_su = Reg(PQ, 256, 512, "su")
            R_cv = Reg(PC, 0, 384, "cv")
            R_pc3 = Reg(PC, 384, 512, "pc3")
            R_T3 = Reg(PT, 0, 384, "T3")
            R_T1 = Reg(PT, 512, 640, "T1")
            R_T2 = Reg(PT, 640, 1024, "T2")

            ident_f = P.sb("ident_f", [128, 128])
            ones_f = P.sb("ones_f", [128, 128])
            maskneg = P.sb("maskneg", [128, 128])
            strict01 = P.sb("strict01", [128, 128])
            triu = P.sb("triu", [128, 128])
            lvlmask = P.sb("lvlmask", [128, 7, 128])
            wsmask = P.sb("wsmask", [128, 128])
            ident_b = P.sb("ident_b", [128, 128], BF16)
            flags = P.sb("flags", [128, 12])
            for tl, src in ((ident_f, k_ident), (ones_f, k_ones), (maskneg, k_maskneg), (strict01, k_strict01),
                            (triu, k_triu), (lvlmask, k_lvlmask), (wsmask, k_wsmask), (flags, flags_in)):
                P.dma(tl[:], src, writes=[tl])
            P.op("vector", lambda e: e.tensor_copy(out=ident_b[:], in_=ident_f[:]), reads=[ident_f], writes=[ident_b])
            epsc = P.sb("epsc", [128, 4])
            P.op("vector", lambda e: e.memset(epsc[:, 0:1], EPS), writes=[epsc])
            P.op("vector", lambda e: e.memset(epsc[:, 1:2], LN_EPS), writes=[epsc])
            P.op("vector", lambda e: e.memset(epsc[:, 2:3], 1.0), writes=[epsc])

            def V(fn, reads, writes):
                P.op("vector", fn, reads, writes)

            def A(fn, reads, writes):
                P.op("scalar", fn, reads, writes)

            def G(fn, reads, writes):
                P.op("gpsimd", fn, reads, writes)

            def MM(out_ap, lhsT, rhs, reads, writes, start=True, stop=True, inc=None):
                if inc is None:
                    inc = stop
                P.op("tensor", lambda e: e.matmul(out_ap, lhsT=lhsT, rhs=rhs, start=start, stop=stop), reads, writes, inc=inc)

            def TR(out_ap, in_ap, ident_ap, reads, writes):
                P.op("tensor", lambda e: e.transpose(out=out_ap, in_=in_ap, identity=ident_ap), reads, writes)

            def rsqrt(dst_ap, v_tl, v_ap, n, writes, shape3=None):
                cache = P.stacks[-1].__dict__.setdefault("rs_tmp", {})
                if n not in cache:
                    cache[n] = (P.sb("rs_i", [128, n], I32), P.sb("rs_a", [128, n]))
                ti, ta = cache[n]
                vi = v_ap.bitcast(I32)
                V(lambda e: e.tensor_copy(out=ta[:], in_=vi), [v_tl], [ta])
                V(lambda e: e.tensor_scalar(out=ta[:], in0=ta[:], scalar1=-0.5, scalar2=float(MAGIC), op0=ALU.mult, op1=ALU.add),
                  [ta], [ta])
                V(lambda e: e.tensor_copy(out=ti[:], in_=ta[:]), [ta], [ti])
                y = ti[:].bitcast(F32)
                for it in range(3):
                    V(lambda e: e.tensor_tensor(out=ta[:], in0=y, in1=y, op=ALU.mult), [ti], [ta])
                    V(lambda e: e.scalar_tensor_tensor(out=ta[:], in0=ta[:], scalar=-0.5, in1=v_ap, op0=ALU.mult,
                                                       op1=ALU.mult), [ta, v_tl], [ta])
                    if it < 2:
                        V(lambda e: e.scalar_tensor_tensor(out=y, in0=ta[:], scalar=1.5, in1=y, op0=ALU.add,
                                                           op1=ALU.mult), [ta, ti], [ti])
                    else:
                        V(lambda e: e.scalar_tensor_tensor(out=dst_ap, in0=ta[:], scalar=1.5, in1=y, op0=ALU.add,
                                                           op1=ALU.mult), [ta, ti], writes)

            wstage = [P.sb("wstage%d" % i, [128, KD, 128]) for i in range(2)]
            wbf = [P.sb("wbf%d" % i, [128, KD, 128], BF16) for i in range(6)]
            wctr = [0, 0]

            def load_w(src2d, ncols=128):
                stg = wstage[wctr[0] % 2]
                wctr[0] += 1
                dst = wbf[wctr[1] % 6]
                wctr[1] += 1
                P.dma(stg[:, :, 0:ncols], src2d.rearrange("(k p) n -> p k n", p=128), writes=[stg])
                G(lambda e: e.tensor_copy(out=dst[:, :, 0:ncols], in_=stg[:, :, 0:ncols]), [stg], [dst])
                return dst

            with P.scope():
                cs = P.sb("cs", [128, KD])
                P.dma(cs[:], c_in.rearrange("(k p) -> p k", p=128), writes=[cs])
                cth = P.sb("cth", [128, KD])
                A(lambda e: e.activation(out=cth[:], in_=cs[:], func=AF.Tanh, scale=0.5), [cs], [cth])
                csl = P.sb("csl", [128, KD])
                V(lambda e: e.scalar_tensor_tensor(out=csl[:], in0=cth[:], scalar=1.0, in1=cs[:], op0=ALU.add, op1=ALU.mult),
                  [cth, cs], [csl])
                V(lambda e: e.tensor_scalar(out=csl[:], in0=csl[:], scalar1=0.5, scalar2=None, op0=ALU.mult), [csl], [csl])
                wa = [P.sb("wa%d" % i, [128, KD, 512]) for i in range(2)]
                mrow = P.sb("mrow", [1, DEPTH, 1536])
                brow = P.sb("brow", [1, DEPTH, 1536])
                P.dma(brow[:, 0:wl, :], b_ada.rearrange("(o l) n -> o l n", o=1), writes=[brow])
                if wl < DEPTH:
                    V(lambda e: e.memset(mrow[:], 0.0), [], [mrow])
                i = 0
                for l in range(wl):
                    for j in range(3):
                        w = wa[i % 2]
                        P.dma(w[:], w_ada[l, :, j * 512:(j + 1) * 512].rearrange("(k p) n -> p k n", p=128), writes=[w])
                        acc = PA[i % 2]
                        for k in range(KD):
                            MM(acc[0:1, :], csl[:, k:k + 1], w[:, k, :], [csl, w], [acc], start=(k == 0), stop=(k == KD - 1))
                        V(lambda e, acc=acc, l=l, j=j: e.tensor_tensor(out=mrow[0:1, l, j * 512:(j + 1) * 512], in0=acc[0:1, :],
                                                                      in1=brow[0:1, l, j * 512:(j + 1) * 512], op=ALU.add),
                          [acc, brow], [mrow])
                        i += 1
                P.dma(mod_snd.ap().rearrange("(o l) n -> o l n", o=1), mrow[:], reads=[mrow], writes=[b_modsnd])
                allgather(mod_snd, mod_rcv, b_modsnd, b_modrcv)
            ck(1)
            mod48 = P.sb("mod48", [128, DEPTH, 48])
            ngs = P.sb("ngs", [128, DEPTH, KD])
            for l in range(DEPTH):
                for j in range(4):
                    P.dma(mod48[:, l, j * 12:(j + 1) * 12], mod_rcv[j * DEPTH + l, :].rearrange("(k p) -> p k", p=128),
                          reads=[b_modrcv], writes=[mod48])
                P.dma(ngs[:, l, :], norm_g[l, :].rearrange("(k p) -> p k", p=128), writes=[ngs])
            gsc = P.sb("gsc", [128, DEPTH, KD])
            V(lambda e: e.scalar_tensor_tensor(out=gsc[:], in0=mod48[:, :, 16:32], scalar=1.0, in1=ngs[:], op0=ALU.add,
                                               op1=ALU.mult), [mod48, ngs], [gsc])

            ck(2)
            with P.scope():
                xtok = [P.sb("xtok%d" % i, [128, D]) for i in range(2)]
                xts = [P.sb("xts%d" % i, [128, KD, 128]) for i in range(2)]
                for t in range(-1, 16):
                    xt_ = xtok[(t + 1) % 2]
                    stg = xts[(t + 1) % 2]
                    if t < 0:
                        ntk, c0 = HALO, 0
                        P.dma(xt_[0:HALO, :], xh_in, writes=[xt_])
                    else:
                        ntk, c0 = 128, HALO + t * 128
                        P.dma(xt_[:], x_in[t * 128:(t + 1) * 128, :], writes=[xt_])
                    for k4 in range(4):
                        acc = PA[k4 % 2]
                        for kk in range(4):
                            k = k4 * 4 + kk
                            TR(acc[:, kk * 128:kk * 128 + ntk], xt_[0:ntk, k * 128:(k + 1) * 128], ident_f[0:ntk, 0:ntk],
                               [xt_, ident_f], [acc])
                        if ntk == 128:
                            V(lambda e, acc=acc, stg=stg, k4=k4: e.tensor_copy(
                                out=stg[:, k4 * 4:(k4 + 1) * 4, :], in_=acc[:].rearrange("p (a b) -> p a b", a=4)),
                              [acc], [stg])
                        else:
                            V(lambda e, acc=acc, stg=stg, k4=k4, ntk=ntk: e.tensor_copy(
                                out=stg[:, k4 * 4:(k4 + 1) * 4, 0:ntk],
                                in_=acc[:].rearrange("p (a b) -> p a b", a=4)[:, :, 0:ntk]), [acc], [stg])
                    P.dma(xT_d[:, c0:c0 + ntk].rearrange("(k p) n -> p k n", p=128), stg[:, :, 0:ntk], reads=[stg],
                          writes=b_xT + [b_xTh])

            ck(3)
            for l in range(nlayers):
                with P.scope():
                    cq = P.sb("cq", [4, 3072])
                    P.dma(cq[:], conv_qkv[l], writes=[cq])
                    cwT = P.sb("cwT", [128, 24, 4])
                    for blk in range(24):
                        rg = R_PS[blk % 4]
                        TR(rg.ap(4), cq[0:4, blk * 128:(blk + 1) * 128], ident_f[0:4, 0:4], [cq, ident_f], [rg])
                        V(lambda e, rg=rg, blk=blk: e.tensor_copy(out=cwT[:, blk, :], in_=rg.ap(4)), [rg], [cwT])
                    cvw = P.sb("cvw", [31, 512])
                    P.dma(cvw[:], cv_w[l], writes=[cvw])
                    cvwT = P.sb("cvwT", [128, 4, 31])
                    for blk in range(4):
                        rg = R_PS[blk % 4]
                        TR(rg.ap(31), cvw[0:31, blk * 128:(blk + 1) * 128], ident_f[0:31, 0:31], [cvw, ident_f], [rg])
                        V(lambda e, rg=rg, blk=blk: e.tensor_copy(out=cvwT[:, blk, :], in_=rg.ap(31)), [rg], [cvwT])
                    pp = P.sb("pp", [128, 16])
                    P.dma(pp[:, 0:4], cv_b[l, :].rearrange("(k p) -> p k", p=128), writes=[pp])
                    P.dma(pp[:, 4:8], cv_ln_g[l, :].rearrange("(k p) -> p k", p=128), writes=[pp])
                    P.dma(pp[:, 8:12], cv_ln_b[l, :].rearrange("(k p) -> p k", p=128), writes=[pp])
                    P.dma(pp[:, 12:13], dn_norm_g[l, :].rearrange("(k p) -> p k", p=128), writes=[pp])
                    V(lambda e: e.tensor_scalar(out=pp[:, 12:13], in0=pp[:, 12:13], scalar1=0.5, scalar2=None, op0=ALU.mult),
                      [pp], [pp])
                    sgg = P.sb("sgg", [128, 512])
                    sgbt = P.sb("sgbt", [128, 512])
                    sgbias = P.sb("sgbias", [128, 512])
                    P.dma(sgg[:], sg_ln_g[l:l + 1, :].partition_broadcast(128), writes=[sgg])
                    P.dma(sgbt[:], sg_ln_b[l:l + 1, :].partition_broadcast(128), writes=[sgbt])
                    P.dma(sgbias[:], sg_b[l:l + 1].rearrange("o g t -> o (g t)").partition_broadcast(128), writes=[sgbias])
                    dtb = P.sb("dtb", [128, 8])
                    nega = P.sb("nega", [128, 8])
                    P.dma(dtb[:], dt_bias[l:l + 1, :].partition_broadcast(128), writes=[dtb])
                    P.dma(nega[:], a_log[l:l + 1, :].partition_broadcast(128), writes=[nega])
                    A(lambda e: e.activation(out=nega[:], in_=nega[:], func=AF.Exp), [nega], [nega])
                    V(lambda e: e.tensor_scalar(out=nega[:], in0=nega[:], scalar1=-1.0, scalar2=None, op0=ALU.mult), [nega], [nega])
                    wsn = P.sb("wsn", [128, 4, 128])
                    P.dma(wsn[:], sg_w[l].rearrange("g t s -> t g s"), writes=[wsn])
                    wsT = P.sb("wsT", [128, 4, 128], BF16)
                    for g4 in range(4):
                        rg = R_PS[g4]
                        TR(rg.ap(), wsn[:, g4, :], ident_f[:], [wsn, ident_f], [rg])
                        V(lambda e, rg=rg, g4=g4: e.tensor_tensor(out=wsT[:, g4, :], in0=rg.ap(), in1=wsmask[:], op=ALU.mult),
                          [rg, wsmask], [wsT])
                    SS = [P.sb("SS%d" % h, [128, 256]) for h in range(NH)]
                    SSb = [P.sb("SSb%d" % h, [128, 256], BF16) for h in range(NH)]
                    for h in range(NH):
                        V(lambda e, h=h: e.memset(SS[h][:, 0:128], 0.0), [], [SS[h]])
                        V(lambda e, h=h: e.tensor_copy(out=SS[h][:, 128:256], in_=ident_f[:]), [ident_f], [SS[h]])
                        V(lambda e, h=h: e.tensor_copy(out=SSb[h][:], in_=SS[h][:]), [SS[h]], [SSb[h]])

                    ck(4)
                    for p in range(NPASS):
                        col0 = p * NTP
                        with P.scope():
                            hT = P.sb("hT", [128, KD, WP], BF16)
                            with P.scope():
                                xk = [P.sb("xk%d" % i, [128, WP]) for i in range(3)]
                                sq = [P.sb("sq%d" % i, [128, WP]) for i in range(2)]
                                for k in range(KD):
                                    xb_ = xk[k % 3]
                                    P.dma(xb_[:], xT_d[k * 128:(k + 1) * 128, col0:col0 + WP], reads=[b_xT[k], b_xTh], writes=[xb_])
                                    s_ = sq[k % 2]
                                    A(lambda e, s_=s_, xb_=xb_: e.activation(out=s_[:], in_=xb_[:], func=AF.Square), [xb_], [s_])
                                    for j in range(3):
                                        MM(PA[j][:, 0:352], ones_f[:], s_[:, j * 352:(j + 1) * 352], [ones_f, s_], [PA[j]],
                                           start=(k == 0), stop=(k == KD - 1), inc=True)
                                msq = P.sb("msq", [128, WP])
                                for j in range(3):
                                    V(lambda e, j=j: e.tensor_scalar(out=msq[:, j * 352:(j + 1) * 352], in0=PA[j][:, 0:352],
                                                                     scalar1=1.0 / D, scalar2=EPS, op0=ALU.mult, op1=ALU.add),
                                      [PA[j]], [msq])
                                rstd = P.sb("rstd", [128, WP])
                                rsqrt(rstd[:], msq, msq[:], WP, [rstd])
                                tn = [P.sb("tn%d" % i, [128, WP]) for i in range(2)]
                                for k in range(KD):
                                    xb_ = xk[k % 3]
                                    P.dma(xb_[:], xT_d[k * 128:(k + 1) * 128, col0:col0 + WP], reads=[b_xT[k], b_xTh], writes=[xb_])
                                    t_ = tn[k % 2]
                                    V(lambda e, t_=t_, xb_=xb_: e.tensor_tensor(out=t_[:], in0=xb_[:], in1=rstd[:], op=ALU.mult),
                                      [xb_, rstd], [t_])
                                    A(lambda e, t_=t_, k=k: e.activation(out=hT[:, k, :], in_=t_[:], func=AF.Identity,
                                                                       scale=gsc[:, l, k:k + 1], bias=mod48[:, l, k:k + 1]),
                                      [t_, gsc, mod48], [hT])

                            ck(5)
                            bgw = load_w(w_in[l, :, C_B:C_B + 16], 16)
                            bgr = R_PA2[0]
                            for t in range(TPP):
                                c0 = HALO + t * 128
                                for k in range(KD):
                                    MM(PA[2][:, t * 16:(t + 1) * 16], hT[:, k, c0:c0 + 128], bgw[:, k, 0:16], [hT, bgw], [bgr],
                                       start=(k == 0), stop=(k == KD - 1))
                            ck(51)
                            bg3 = PA[2][:, 0:128].rearrange("p (t c) -> p t c", c=16)
                            beta = P.sb("beta", [128, TPP, 8])
                            A(lambda e: e.activation(out=beta[:], in_=bg3[:, :, 0:8], func=AF.Tanh, scale=0.5), [bgr], [beta])
                            V(lambda e: e.tensor_scalar(out=beta[:], in0=beta[:], scalar1=0.5, scalar2=0.5, op0=ALU.mult, op1=ALU.add),
                              [beta], [beta])
                            ck(52)
                            gg = P.sb("gg", [128, TPP, 8])
                            V(lambda e: e.tensor_tensor(out=gg[:], in0=bg3[:, :, 8:16],
                                                        in1=dtb[:].unsqueeze(1).broadcast_to([128, TPP, 8]), op=ALU.add),
                              [bgr, dtb], [gg])
                            A(lambda e: e.activation(out=gg[:], in_=gg[:], func=AF.Exp), [gg], [gg])
                            A(lambda e: e.activation(out=gg[:], in_=gg[:], func=(AF.Identity if LNTEST else AF.Ln), bias=epsc[:, 2:3]), [gg, epsc], [gg])
                            V(lambda e: e.tensor_tensor(out=gg[:], in0=gg[:], in1=nega[:].unsqueeze(1).broadcast_to([128, TPP, 8]),
                                                        op=ALU.mult), [gg, nega], [gg])
                            ck(53)
                            gg2 = gg[:].rearrange("p t h -> p (t h)")
                            r_gc, r_gs = R_PA2[1], R_PA2[2]
                            MM(r_gc.ap(64), triu[:], gg2, [triu, gg], [r_gc])
                            MM(r_gs.ap(64), ones_f[:], gg2, [ones_f, gg], [r_gs])
                            ck(54)
                            gc = P.sb("gc", [128, TPP, 8])
                            egc = P.sb("egc", [128, TPP, 8])
                            negegc = P.sb("negegc", [128, TPP, 8])
                            edl = P.sb("edl", [128, TPP, 8])
                            egl = P.sb("egl", [128, TPP, 8])

                            def f2(tl):
                                return tl[:].rearrange("p t h -> p (t h)")
                            V(lambda e: e.tensor_copy(out=f2(gc), in_=r_gc.ap(64)), [r_gc], [gc])
                            ck(551)
                            A(lambda e: e.activation(out=f2(egc), in_=f2(gc), func=AF.Exp), [gc], [egc])
                            ck(55)
                            V(lambda e: e.tensor_scalar(out=f2(negegc), in0=f2(egc), scalar1=-1.0, scalar2=None, op0=ALU.mult),
                              [egc], [negegc])
                            V(lambda e: e.tensor_copy(out=f2(egl), in_=r_gs.ap(64)), [r_gs], [egl])
                            V(lambda e: e.tensor_tensor(out=f2(edl), in0=f2(egl), in1=f2(gc), op=ALU.subtract), [egl, gc], [edl])
                            ck(56)
                            A(lambda e: e.activation(out=f2(edl), in_=f2(edl), func=AF.Exp), [edl], [edl])
                            A(lambda e: e.activation(out=f2(egl), in_=f2(egl), func=AF.Exp), [egl], [egl])

                            ck(6)
                            with P.scope():
                                pT = [P.sb("pT%d" % i, [128, WP], BF16) for i in range(3)]
                                dg = P.sb("dg", [128, 12, 128], BF16)
                                th3 = P.sb("th3", [128, 384])
                                s2 = P.sb("s2", [128, 384])
                                junk = P.sb("junk", [128, 128])
                                ssq = P.sb("ssq", [128, 2])
                                rn = P.sb("rn", [128, 2])
                                qn = P.sb("qn", [128, 128], BF16)
                                kn = P.sb("kn", [128, 128], BF16)
                                qe = P.sb("qe", [128, 128], BF16)
                                kd = P.sb("kd", [128, 128], BF16)
                                vv = P.sb("vv", [128, 256])
                                V(lambda e: e.memset(vv[:, 128:256], 0.0), [], [vv])
                                T3 = P.sb("T3", [128, 3, 128], BF16)
                                gdiag = P.sb("gdiag", [128, 128])
                                gtmp = P.sb("gtmp", [128, 128])
                                Gm = P.sb("Gm", [128, 128])
                                GmS = P.sb("GmS", [128, 128])
                                qkT = P.sb("qkT", [128, 128], BF16)
                                Up = P.sb("Up", [128, 128])
                                Off = P.sb("Off", [128, 7, 128])
                                Eb = [P.sb("E%d" % i, [128, 128]) for i in range(2)]
                                Fb = [P.sb("F%d" % i, [128, 128]) for i in range(2)]
                                ZT = P.sb("ZT", [128, 128])
                                Ebf = P.sb("Ebf", [128, 128], BF16)
                                rr = P.sb("rr", [128, 256], BF16)
                                vnew = P.sb("vnew", [128, 256], BF16)
                                ol = P.sb("ol", [128, 128])
                                qts = P.sb("qts", [128, 128], BF16)
                                zth = P.sb("zth", [128, 512])
                                zs = P.sb("zs", [128, 512], BF16)
                                for h in range(NH):
                                    wq = load_w(w_in[l, :, C_Q + h * 128:C_Q + (h + 1) * 128])
                                    wk = load_w(w_in[l, :, C_K + h * 128:C_K + (h + 1) * 128])
                                    wv = load_w(w_in[l, :, C_V + h * 128:C_V + (h + 1) * 128])
                                    wz = load_w(w_in[l, :, C_Z + h * 128:C_Z + (h + 1) * 128])
                                    i = 0
                                    for X, w_ in enumerate((wq, wk, wv)):
                                        for j in range(3):
                                            acc = PA[i % 2]
                                            i += 1
                                            for k in range(KD):
                                                MM(acc[:, 0:352], w_[:, k, :], hT[:, k, j * 352:(j + 1) * 352], [w_, hT], [acc],
                                                   start=(k == 0), stop=(k == KD - 1))
                                            if i % 2 == 0:
                                                V(lambda e, acc=acc, X=X, j=j: e.tensor_copy(out=pT[X][:, j * 352:(j + 1) * 352],
                                                                                             in_=acc[:, 0:352]), [acc], [pT[X]])
                                            else:
                                                A(lambda e, acc=acc, X=X, j=j: e.activation(out=pT[X][:, j * 352:(j + 1) * 352],
                                                                                            in_=acc[:, 0:352], func=AF.Copy),
                                                  [acc], [pT[X]])
                                        if p == 0:
                                            G(lambda e, X=X: e.tensor_scalar(out=pT[X][:, 0:HALO], in0=pT[X][:, 0:HALO],
                                                                             scalar1=flags[:, 8:9], scalar2=None, op0=ALU.mult),
                                              [pT[X], flags], [pT[X]])
                                    if h == 0:
                                        ck(61)
                                    for j in range(2):
                                        acc = PA[j % 2]
                                        n0 = HALO + j * 512
                                        for k in range(KD):
                                            MM(acc[:], wz[:, k, :], hT[:, k, n0:n0 + 512], [wz, hT], [acc],
                                               start=(k == 0), stop=(k == KD - 1))
                                        A(lambda e, acc=acc: e.activation(out=zth[:], in_=acc[:], func=AF.Tanh, scale=0.5), [acc], [zth])
                                        V(lambda e, acc=acc: e.scalar_tensor_tensor(out=zs[:], in0=zth[:], scalar=1.0, in1=acc[:],
                                                                                    op0=ALU.add, op1=ALU.mult), [zth, acc], [zs])
                                        tok0 = p * NTP + j * 512
                                        P.dma(szT_d[h, :, tok0:tok0 + 512], zs[:], reads=[zs], writes=[b_szT])
                                    if h == 0:
                                        ck(62)
                                    for X in range(3):
                                        for j in range(4):
                                            G(lambda e, X=X, j=j, h=h: e.tensor_scalar(
                                                out=dg[:, X * 4 + j, :], in0=ident_b[:], scalar1=cwT[:, X * 8 + h, j:j + 1],
                                                scalar2=None, op0=ALU.mult), [ident_b, cwT], [dg])
                                    if h == 0:
                                        ck(63)
                                    for t in range(TPP):
                                        c0 = HALO + t * 128
                                        T = p * TPP + t
                                        bcol = beta[:, t, h:h + 1]
                                        for X in range(3):
                                            for j in range(4):
                                                MM(PC[:, X * 128:(X + 1) * 128], pT[X][:, c0 - 3 + j:c0 - 3 + j + 128], dg[:, X * 4 + j, :],
                                                   [pT[X], dg], [R_cv], start=(j == 0), stop=(j == 3))
                                        A(lambda e: e.activation(out=th3[:], in_=R_cv.ap(), func=AF.Tanh, scale=0.5), [R_cv], [th3])
                                        V(lambda e: e.scalar_tensor_tensor(out=s2[:], in0=th3[:], scalar=1.0, in1=R_cv.ap(),
                                                                           op0=ALU.add, op1=ALU.mult), [th3, R_cv], [s2])
                                        A(lambda e: e.activation(out=junk[:], in_=s2[:, 0:128], func=AF.Square, accum_out=ssq[:, 0:1]),
                                          [s2], [junk, ssq])
                                        A(lambda e: e.activation(out=junk[:], in_=s2[:, 128:256], func=AF.Square, accum_out=ssq[:, 1:2]),
                                          [s2], [junk, ssq])
                                        V(lambda e: e.tensor_scalar(out=ssq[:], in0=ssq[:], scalar1=4.0 * EPS, scalar2=None, op0=ALU.add),
                                          [ssq], [ssq])
                                        rsqrt(rn[:], ssq, ssq[:], 2, [rn])
                                        V(lambda e: e.tensor_scalar(out=qn[:], in0=s2[:, 0:128], scalar1=rn[:, 0:1], scalar2=128.0 ** -0.5,
                                                                    op0=ALU.mult, op1=ALU.mult), [s2, rn], [qn])
                                        V(lambda e: e.tensor_scalar(out=kn[:], in0=s2[:, 128:256], scalar1=rn[:, 1:2], scalar2=None,
                                                                    op0=ALU.mult), [s2, rn], [kn])
                                        G(lambda e, t=t, h=h: e.tensor_scalar(out=qe[:], in0=qn[:], scalar1=egc[:, t, h:h + 1], scalar2=None,
                                                                              op0=ALU.mult), [qn, egc], [qe])
                                        G(lambda e, t=t, h=h: e.tensor_scalar(out=kd[:], in0=kn[:], scalar1=edl[:, t, h:h + 1], scalar2=None,
                                                                              op0=ALU.mult), [kn, edl], [kd])
                                        G(lambda e: e.tensor_scalar(out=vv[:, 0:128], in0=s2[:, 256:384], scalar1=0.5, scalar2=None,
                                                                    op0=ALU.mult), [s2], [vv])
                                        if h == 0 and t == 0:
                                            ck(64)
                                        TR(PT[:, 0:128], qn[:], ident_b[:], [qn, ident_b], [R_T3])
                                        TR(PT[:, 128:256], qe[:], ident_b[:], [qe, ident_b], [R_T3])
                                        TR(PT[:, 256:384], kn[:], ident_b[:], [kn, ident_b], [R_T3])
                                        A(lambda e: e.activation(out=T3[:].rearrange("p a b -> p (a b)"), in_=R_T3.ap(), func=AF.Copy),
                                          [R_T3], [T3])
                                        qnT, qeT, knT = T3[:, 0, :], T3[:, 1, :], T3[:, 2, :]
                                        r_kk, r_qk, r_gr = R_PA2[0], R_PA2[1], R_PA2[2]
                                        MM(r_kk.ap(), knT, knT, [T3], [r_kk])
                                        MM(r_qk.ap(), knT, qnT, [T3], [r_qk])
                                        if h == 0 and t == 0:
                                            ck(65)
                                        gcol = gc[:, t, h:h + 1]
                                        G(lambda e, gcol=gcol: e.tensor_scalar(out=gdiag[:], in0=ident_f[:], scalar1=gcol, scalar2=None,
                                                                               op0=ALU.mult), [ident_f, gc], [gdiag])
                                        MM(r_gr.ap(), ones_f[:], gdiag[:], [ones_f, gdiag], [r_gr])
                                        V(lambda e, gcol=gcol: e.scalar_tensor_tensor(out=gtmp[:], in0=r_gr.ap(), scalar=gcol, in1=maskneg[:],
                                                                                      op0=ALU.subtract, op1=ALU.add),
                                          [r_gr, gc, maskneg], [gtmp])
                                        A(lambda e: e.activation(out=Gm[:], in_=gtmp[:], func=AF.Exp), [gtmp], [Gm])
                                        G(lambda e: e.tensor_tensor(out=GmS[:], in0=Gm[:], in1=strict01[:], op=ALU.mult), [Gm, strict01], [GmS])
                                        V(lambda e: e.tensor_tensor(out=qkT[:], in0=r_qk.ap(), in1=Gm[:], op=ALU.mult), [r_qk, Gm], [qkT])
                                        V(lambda e, bcol=bcol: e.scalar_tensor_tensor(out=Up[:], in0=r_kk.ap(), scalar=bcol, in1=GmS[:],
                                                                                      op0=ALU.mult, op1=ALU.mult), [r_kk, beta, GmS], [Up])
                                        if h == 0 and t == 0:
                                            ck(66)
                                        G(lambda e: e.tensor_tensor(out=Off[:], in0=Up[:].unsqueeze(1).broadcast_to([128, 7, 128]),
                                                                    in1=lvlmask[:], op=ALU.mult), [Up, lvlmask], [Off])
                                        E, Fm = Eb[0], Fb[0]
                                        G(lambda e, E=E: e.tensor_tensor(out=E[:], in0=ident_f[:], in1=Off[:, 0, :], op=ALU.subtract),
                                          [ident_f, Off], [E])
                                        TR(R_PS[0].ap(), Off[:, 0, :], ident_f[:], [Off, ident_f], [R_PS[0]])
                                        V(lambda e, Fm=Fm: e.tensor_tensor(out=Fm[:], in0=ident_f[:], in1=R_PS[0].ap(), op=ALU.subtract),
                                          [ident_f, R_PS[0]], [Fm])
                                        cur = 0
                                        for lev in range(1, 7):
                                            E, Fm = Eb[cur], Fb[cur]
                                            En, Fn = Eb[1 - cur], Fb[1 - cur]
                                            MM(R_PS[1].ap(), Off[:, lev, :], Fm[:], [Off, Fm], [R_PS[1]])
                                            A(lambda e: e.activation(out=ZT[:], in_=R_PS[1].ap(), func=AF.Copy), [R_PS[1]], [ZT])
                                            MM(R_PS[2].ap(), ZT[:], E[:], [ZT, E], [R_PS[2]])
                                            if lev < 6:
                                                MM(R_PS[3].ap(), E[:], ZT[:], [E, ZT], [R_PS[3]])
                                                V(lambda e, E=E, En=En: e.tensor_tensor(out=En[:], in0=E[:], in1=R_PS[2].ap(), op=ALU.subtract),
                                                  [E, R_PS[2]], [En])
                                                V(lambda e, Fm=Fm, Fn=Fn: e.tensor_tensor(out=Fn[:], in0=Fm[:], in1=R_PS[3].ap(), op=ALU.subtract),
                                                  [Fm, R_PS[3]], [Fn])
                                            else:
                                                V(lambda e, E=E: e.tensor_tensor(out=Ebf[:], in0=E[:], in1=R_PS[2].ap(), op=ALU.subtract),
                                                  [E, R_PS[2]], [Ebf])
                                            cur = 1 - cur
                                        if h == 0 and t == 0:
                                            ck(67)
                                        S_, Sb_ = SS[h], SSb[h]
                                        MM(R_ks.ap(), knT, Sb_[:], [T3, Sb_], [R_ks])
                                        V(lambda e, t=t, h=h: e.scalar_tensor_tensor(out=rr[:], in0=R_ks.ap(), scalar=negegc[:, t, h:h + 1],
                                                                                     in1=vv[:], op0=ALU.mult, op1=ALU.add),
                                          [R_ks, negegc, vv], [rr])
                                        MM(R_xr.ap(), Ebf[:], rr[:], [Ebf, rr], [R_xr])
                                        A(lambda e, bcol=bcol: e.activation(out=vnew[:], in_=R_xr.ap(), func=AF.Copy, scale=bcol),
                                          [R_xr, beta], [vnew])
                                        MM(R_o.ap(), qeT, Sb_[:, 0:128], [T3, Sb_], [R_o], start=True, stop=False)
                                        MM(R_o.ap(), qkT[:], vnew[:, 0:128], [qkT, vnew], [R_o], start=False, stop=True)
                                        MM(R_qt.ap(), Sb_[:, 128:256], qeT, [T3, Sb_], [R_qt], start=True, stop=False)
                                        MM(R_qt.ap(), vnew[:, 128:256], qkT[:], [qkT, vnew], [R_qt], start=False, stop=True)
                                        MM(R_su.ap(), kd[:], vnew[:], [kd, vnew], [R_su])
                                        A(lambda e: e.activation(out=ol[:], in_=R_o.ap(), func=AF.Copy), [R_o], [ol])
                                        V(lambda e: e.tensor_copy(out=qts[:], in_=R_qt.ap()), [R_qt], [qts])
                                        P.dma(olocal_d[T, h], ol[:], reads=[ol], writes=[b_olocal])
                                        P.dma(qtT_d[T, h], qts[:], reads=[qts], writes=[b_qtT])
                                        V(lambda e, t=t, h=h, S_=S_: e.scalar_tensor_tensor(out=S_[:], in0=S_[:], scalar=egl[:, t, h:h + 1],
                                                                                            in1=R_su.ap(), op0=ALU.mult, op1=ALU.add),
                                          [S_, egl, R_su], [S_])
                                        A(lambda e, S_=S_, Sb_=Sb_: e.activation(out=Sb_[:], in_=S_[:], func=AF.Copy), [S_], [Sb_])

                            ck(7)
                            with P.scope():
                                guT = P.sb("guT", [128, 4, NTP], BF16)
                                sgT = P.sb("sgT", [128, 4, NTP], BF16)
                                gth = P.sb("gth", [128, 512])
                                for g4 in range(4):
                                    wu = load_w(w_in[l, :, C_USG + g4 * 128:C_USG + (g4 + 1) * 128])
                                    for j in range(2):
                                        acc = PA[j % 2]
                                        n0 = HALO + j * 512
                                        for k in range(KD):
                                            MM(acc[:], wu[:, k, :], hT[:, k, n0:n0 + 512], [wu, hT], [acc], start=(k == 0), stop=(k == KD - 1))
                                        A(lambda e, acc=acc, g4=g4, j=j: e.activation(out=guT[:, g4, j * 512:(j + 1) * 512], in_=acc[:],
                                                                                      func=AF.Gelu), [acc], [guT])
                                vg = P.sb("vg", [128, TPP, 512])
                                wvs = [load_w(w_in[l, :, C_VSG + g4 * 128:C_VSG + (g4 + 1) * 128]) for g4 in range(4)]
                                for t in range(TPP):
                                    c0 = HALO + t * 128
                                    acc = PA[t % 2]
                                    for g4 in range(4):
                                        for k in range(KD):
                                            MM(acc[:, g4 * 128:(g4 + 1) * 128], hT[:, k, c0:c0 + 128], wvs[g4][:, k, :], [hT, wvs[g4]], [acc],
                                               start=(k == 0), stop=(k == KD - 1))
                                    A(lambda e, acc=acc, t=t: e.activation(out=vg[:, t, :], in_=acc[:], func=AF.Gelu), [acc], [vg])
                                for g4 in range(4):
                                    wg_ = load_w(w_in[l, :, C_GSG + g4 * 128:C_GSG + (g4 + 1) * 128])
                                    for j in range(2):
                                        acc = PA[j % 2]
                                        n0 = HALO + j * 512
                                        for k in range(KD):
                                            MM(acc[:], wg_[:, k, :], hT[:, k, n0:n0 + 512], [wg_, hT], [acc], start=(k == 0), stop=(k == KD - 1))
                                        A(lambda e, acc=acc: e.activation(out=gth[:], in_=acc[:], func=AF.Tanh, scale=0.5), [acc], [gth])
                                        V(lambda e, acc=acc, g4=g4, j=j: e.scalar_tensor_tensor(out=sgT[:, g4, j * 512:(j + 1) * 512], in0=gth[:],
                                                                                                scalar=1.0, in1=acc[:], op0=ALU.add, op1=ALU.mult),
                                          [gth, acc], [sgT])
                                V(lambda e: e.scalar_tensor_tensor(out=guT[:], in0=guT[:], scalar=0.5, in1=sgT[:], op0=ALU.mult, op1=ALU.mult),
                                  [guT, sgT], [guT])
                                st6 = P.sb("st6", [128, 4, 6])
                                mv = P.sb("mv", [128, 4, 2])
                                vr = P.sb("vr", [128, 4])
                                rsd = P.sb("rsd", [128, 4])
                                vln = P.sb("vln", [128, 512])
                                vlb = P.sb("vlb", [128, 512], BF16)
                                ysg = P.sb("ysg", [128, 4, 128])
                                ysb = P.sb("ysb", [128, 4, 128], BF16)
                                for t in range(TPP):
                                    for g4 in range(4):
                                        V(lambda e, t=t, g4=g4: e.bn_stats(out=st6[:, g4, :], in_=vg[:, t, g4 * 128:(g4 + 1) * 128]), [vg], [st6])
                                        V(lambda e, g4=g4: e.bn_aggr(out=mv[:, g4, :], in_=st6[:, g4, :]), [st6], [mv])
                                    V(lambda e: e.tensor_scalar(out=vr[:], in0=mv[:, :, 1], scalar1=LN_EPS, scalar2=None, op0=ALU.add), [mv], [vr])
                                    rsqrt(rsd[:], vr, vr[:], 4, [rsd])
                                    for g4 in range(4):
                                        V(lambda e, t=t, g4=g4: e.tensor_scalar(out=vln[:, g4 * 128:(g4 + 1) * 128], in0=vg[:, t, g4 * 128:(g4 + 1) * 128],
                                                                                scalar1=mv[:, g4, 0:1], scalar2=rsd[:, g4:g4 + 1],
                                                                                op0=ALU.subtract, op1=ALU.mult), [vg, mv, rsd], [vln])
                                    G(lambda e: e.tensor_tensor(out=vln[:], in0=vln[:], in1=sgg[:], op=ALU.mult), [vln, sgg], [vln])
                                    G(lambda e: e.tensor_tensor(out=vlb[:], in0=vln[:], in1=sgbt[:], op=ALU.add), [vln, sgbt], [vlb])
                                    acc = PA[t % 2]
                                    for g4 in range(4):
                                        MM(acc[:, g4 * 128:(g4 + 1) * 128], vlb[:, g4 * 128:(g4 + 1) * 128], wsT[:, g4, :], [vlb, wsT], [acc])
                                    V(lambda e, acc=acc: e.tensor_tensor(out=ysg[:].rearrange("p a b -> p (a b)"), in0=acc[:], in1=sgbias[:],
                                                                         op=ALU.add), [acc, sgbias], [ysg])
                                    V(lambda e, t=t: e.tensor_tensor(out=ysb[:], in0=ysg[:], in1=guT[:, :, t * 128:(t + 1) * 128], op=ALU.mult),
                                      [ysg, guT], [ysb])
                                    tok0 = p * NTP + t * 128
                                    P.dma(yscT_d[0:4, :, tok0:tok0 + 128].rearrange("j c n -> c j n"), ysb[:], reads=[ysb], writes=[b_yscT])

                            ck(8)
                            with P.scope():
                                gluT = P.sb("gluT", [128, 4, WP], BF16)
                                gcT = P.sb("gcT", [128, 4, NTP], BF16)
                                cth = P.sb("cth", [128, 512])
                                csg = P.sb("csg", [128, 512])
                                dgc = P.sb("dgc", [128, 31, 128], BF16)
                                for i4 in range(4):
                                    wa_ = load_w(w_in[l, :, C_ACV + i4 * 128:C_ACV + (i4 + 1) * 128])
                                    wb_ = load_w(w_in[l, :, C_BCV + i4 * 128:C_BCV + (i4 + 1) * 128])
                                    for j in range(3):
                                        for k in range(KD):
                                            MM(PA[0][:, 0:352], wa_[:, k, :], hT[:, k, j * 352:(j + 1) * 352], [wa_, hT], [PA[0]],
                                               start=(k == 0), stop=(k == KD - 1))
                                        for k in range(KD):
                                            MM(PA[1][:, 0:352], wb_[:, k, :], hT[:, k, j * 352:(j + 1) * 352], [wb_, hT], [PA[1]],
                                               start=(k == 0), stop=(k == KD - 1))
                                        A(lambda e: e.activation(out=cth[:, 0:352], in_=PA[1][:, 0:352], func=AF.Tanh, scale=0.5), [PA[1]], [cth])
                                        V(lambda e: e.tensor_scalar(out=cth[:, 0:352], in0=cth[:, 0:352], scalar1=0.5, scalar2=0.5,
                                                                    op0=ALU.mult, op1=ALU.add), [cth], [cth])
                                        V(lambda e, i4=i4, j=j: e.tensor_tensor(out=gluT[:, i4, j * 352:(j + 1) * 352], in0=PA[0][:, 0:352],
                                                                                in1=cth[:, 0:352], op=ALU.mult), [PA[0], cth], [gluT])
                                    if p == 0:
                                        G(lambda e, i4=i4: e.tensor_scalar(out=gluT[:, i4, 0:HALO], in0=gluT[:, i4, 0:HALO], scalar1=flags[:, 8:9],
                                                                           scalar2=None, op0=ALU.mult), [gluT, flags], [gluT])
                                    wg_ = load_w(w_in[l, :, C_GCV + i4 * 128:C_GCV + (i4 + 1) * 128])
                                    for j in range(2):
                                        acc = PA[j % 2]
                                        n0 = HALO + j * 512
                                        for k in range(KD):
                                            MM(acc[:], wg_[:, k, :], hT[:, k, n0:n0 + 512], [wg_, hT], [acc], start=(k == 0), stop=(k == KD - 1))
                                        A(lambda e, acc=acc: e.activation(out=csg[:], in_=acc[:], func=AF.Tanh, scale=0.5), [acc], [csg])
                                        V(lambda e, acc=acc, i4=i4, j=j: e.scalar_tensor_tensor(out=gcT[:, i4, j * 512:(j + 1) * 512], in0=csg[:],
                                                                                                scalar=1.0, in1=acc[:], op0=ALU.add, op1=ALU.mult),
                                          [csg, acc], [gcT])
                                dw = P.sb("dw", [128, 512])
                                dw2 = P.sb("dw2", [128, 512])
                                mean = P.sb("mean", [128, 512])
                                var = P.sb("var", [128, 512])
                                rsv = P.sb("rsv", [128, 512])
                                ycv = P.sb("ycv", [128, 512], BF16)
                                for i4 in range(4):
                                    for j in range(31):
                                        G(lambda e, i4=i4, j=j: e.tensor_scalar(out=dgc[:, j, :], in0=ident_b[:], scalar1=cvwT[:, i4, j:j + 1],
                                                                                scalar2=None, op0=ALU.mult), [ident_b, cvwT], [dgc])
                                    for n in range(2):
                                        c0 = HALO + n * 512
                                        acc = PA[n % 2]
                                        for j in range(31):
                                            MM(acc[:], dgc[:, j, :], gluT[:, i4, c0 - 30 + j:c0 - 30 + j + 512], [dgc, gluT], [acc],
                                               start=(j == 0), stop=(j == 30))
                                        A(lambda e, acc=acc, i4=i4: e.activation(out=dw[:], in_=acc[:], func=AF.Identity, bias=pp[:, i4:i4 + 1]),
                                          [acc, pp], [dw])
                                        A(lambda e: e.activation(out=dw2[:], in_=dw[:], func=AF.Square), [dw], [dw2])
                                        MM(PC[:], ones_f[:], dw[:], [ones_f, dw], [R_cv, R_pc3])
                                        MM(PA[2][:], ones_f[:], dw2[:], [ones_f, dw2], [R_PA2[0], R_PA2[1], R_PA2[2], R_PA2[3]])
                                        A(lambda e: e.activation(out=mean[:], in_=PC[:], func=AF.Copy, scale=1.0 / 128), [R_cv, R_pc3], [mean])
                                        G(lambda e: e.tensor_tensor(out=var[:], in0=mean[:], in1=mean[:], op=ALU.mult), [mean], [var])
                                        V(lambda e: e.scalar_tensor_tensor(out=var[:], in0=PA[2][:], scalar=1.0 / 128, in1=var[:], op0=ALU.mult,
                                                                           op1=ALU.subtract), [R_PA2[0], R_PA2[1], R_PA2[2], R_PA2[3], var], [var])
                                        V(lambda e: e.tensor_scalar(out=var[:], in0=var[:], scalar1=LN_EPS, scalar2=None, op0=ALU.add), [var], [var])
                                        rsqrt(rsv[:], var, var[:], 512, [rsv])
                                        G(lambda e: e.tensor_tensor(out=dw[:], in0=dw[:], in1=mean[:], op=ALU.subtract), [dw, mean], [dw])
                                        V(lambda e: e.tensor_tensor(out=dw[:], in0=dw[:], in1=rsv[:], op=ALU.mult), [dw, rsv], [dw])
                                        A(lambda e, i4=i4: e.activation(out=dw[:], in_=dw[:], func=AF.Identity, scale=pp[:, 4 + i4:5 + i4],
                                                                        bias=pp[:, 8 + i4:9 + i4]), [dw, pp], [dw])
                                        A(lambda e: e.activation(out=dw2[:], in_=dw[:], func=AF.Tanh, scale=0.5), [dw], [dw2])
                                        V(lambda e: e.scalar_tensor_tensor(out=dw2[:], in0=dw2[:], scalar=1.0, in1=dw[:], op0=ALU.add, op1=ALU.mult),
                                          [dw2, dw], [dw2])
                                        V(lambda e, i4=i4, n=n: e.scalar_tensor_tensor(out=ycv[:], in0=dw2[:], scalar=0.25,
                                                                                       in1=gcT[:, i4, n * 512:(n + 1) * 512], op0=ALU.mult,
                                                                                       op1=ALU.mult), [dw2, gcT], [ycv])
                                        tok0 = p * NTP + n * 512
                                        P.dma(yscT_d[4 + i4, :, tok0:tok0 + 512], ycv[:], reads=[ycv], writes=[b_yscT])

                    ck(9)
                    for h in range(NH):
                        P.dma(st_snd[h * 128:(h + 1) * 128, :], SS[h][:], reads=[SS[h]], writes=[b_stsnd])
                    allgather(st_snd, st_rcv, b_stsnd, b_strcv)
                    ck(10)
                    with P.scope():
                        Sin = P.sb("Sin", [128, NH, 128])
                        Sib = P.sb("Sib", [128, NH, 128], BF16)
                        V(lambda e: e.memset(Sin[:], 0.0), [], [Sin])
                        BM = [P.sb("BM%d" % i, [128, 256]) for i in range(2)]
                        ShT = P.sb("ShT", [128, 128])
                        Snw = P.sb("Snw", [128, 128])
                        i = 0
                        for h in range(NH):
                            for j in range(3):
                                bm = BM[i % 2]
                                i += 1
                                r0 = (j * NH + h) * 128
                                P.dma(bm[:], st_rcv[r0:r0 + 128, :], reads=[b_strcv], writes=[bm])
                                if j == 0:
                                    V(lambda e, bm=bm: e.tensor_copy(out=Snw[:], in_=bm[:, 0:128]), [bm], [Snw])
                                else:
                                    TR(R_PS[0].ap(), bm[:, 128:256], ident_f[:], [bm, ident_f], [R_PS[0]])
                                    V(lambda e: e.tensor_copy(out=ShT[:], in_=R_PS[0].ap()), [R_PS[0]], [ShT])
                                    MM(R_PS[1].ap(), ShT[:], Sin[:, h, :], [ShT, Sin], [R_PS[1]])
                                    V(lambda e, bm=bm: e.tensor_tensor(out=Snw[:], in0=R_PS[1].ap(), in1=bm[:, 0:128], op=ALU.add),
                                      [R_PS[1], bm], [Snw])
                                V(lambda e, h=h: e.tensor_tensor(out=Snw[:], in0=Snw[:], in1=Sin[:, h, :], op=ALU.subtract), [Snw, Sin], [Snw])
                                V(lambda e, h=h, j=j: e.scalar_tensor_tensor(out=Sin[:, h, :], in0=Snw[:], scalar=flags[:, j:j + 1], in1=Sin[:, h, :],
                                                                             op0=ALU.mult, op1=ALU.add), [Snw, flags, Sin], [Sin])
                        V(lambda e: e.tensor_copy(out=Sib[:], in_=Sin[:]), [Sin], [Sib])

                        for p in range(NPASS):
                            with P.scope():
                                yT = P.sb("yT", [128, KD, NTP], BF16)
                                P.dma(yT[:, 8:16, :], yscT_d[:, :, p * NTP:(p + 1) * NTP].rearrange("j c n -> c j n"), reads=[b_yscT], writes=[yT])
                                olb = [P.sb("olb%d" % i, [128, TPP, 128]) for i in range(2)]
                                qtb = [P.sb("qtb%d" % i, [128, TPP, 128], BF16) for i in range(2)]
                                szb = [P.sb("szb%d" % i, [128, NTP], BF16) for i in range(2)]
                                oc = P.sb("oc", [128, TPP, 128])
                                junk2 = P.sb("junk2", [128, 128])
                                sso = P.sb("sso", [128, TPP])
                                rso = P.sb("rso", [128, TPP])
                                onb = P.sb("onb", [128, 128], BF16)
                                for h in range(NH):
                                    ol_, qt_, sz_ = olb[h % 2], qtb[h % 2], szb[h % 2]
                                    P.dma(ol_[:], olocal_d[p * TPP:(p + 1) * TPP, h].rearrange("t c d -> c t d"), reads=[b_olocal], writes=[ol_])
                                    P.dma(qt_[:], qtT_d[p * TPP:(p + 1) * TPP, h].rearrange("t k c -> k t c"), reads=[b_qtT], writes=[qt_])
                                    P.dma(sz_[:], szT_d[h, :, p * NTP:(p + 1) * NTP], reads=[b_szT], writes=[sz_])
                                    for t in range(TPP):
                                        rg = R_PS[t % 4]
                                        MM(rg.ap(), qt_[:, t, :], Sib[:, h, :], [qt_, Sib], [rg])
                                        V(lambda e, rg=rg, t=t, ol_=ol_: e.tensor_tensor(out=oc[:, t, :], in0=rg.ap(), in1=ol_[:, t, :], op=ALU.add),
                                          [rg, ol_], [oc])
                                        A(lambda e, t=t: e.activation(out=junk2[:], in_=oc[:, t, :], func=AF.Square, accum_out=sso[:, t:t + 1]),
                                          [oc], [junk2, sso])
                                    V(lambda e: e.tensor_scalar(out=sso[:], in0=sso[:], scalar1=1.0 / 128, scalar2=EPS, op0=ALU.mult, op1=ALU.add),
                                      [sso], [sso])
                                    rsqrt(rso[:], sso, sso[:], TPP, [rso])
                                    for t in range(TPP):
                                        V(lambda e, t=t: e.tensor_scalar(out=onb[:], in0=oc[:, t, :], scalar1=rso[:, t:t + 1], scalar2=None, op0=ALU.mult),
                                          [oc, rso], [onb])
                                        TR(R_T1.ap(), onb[:], ident_b[:], [onb, ident_b], [R_T1])
                                        V(lambda e, t=t, h=h, sz_=sz_: e.scalar_tensor_tensor(out=yT[:, h, t * 128:(t + 1) * 128], in0=R_T1.ap(),
                                                                                              scalar=pp[:, 12:13], in1=sz_[:, t * 128:(t + 1) * 128],
                                                                                              op0=ALU.mult, op1=ALU.mult), [R_T1, pp, sz_], [yT])
                                xc = [P.sb("xc%d" % i, [128, NTP]) for i in range(2)]
                                for j in range(KD):
                                    wo = load_w(w_out[l, :, j * 128:(j + 1) * 128])
                                    xc_ = xc[j % 2]
                                    cA = HALO + p * NTP
                                    P.dma(xc_[:], xT_d[j * 128:(j + 1) * 128, cA:cA + NTP], reads=[b_xT[j]], writes=[xc_])
                                    for n in range(2):
                                        acc = PA[n % 2]
                                        for k in range(KD):
                                            MM(acc[:], wo[:, k, :], yT[:, k, n * 512:(n + 1) * 512], [wo, yT], [acc], start=(k == 0), stop=(k == KD - 1))
                                        V(lambda e, acc=acc, xc_=xc_, j=j, n=n: e.scalar_tensor_tensor(
                                            out=xc_[:, n * 512:(n + 1) * 512], in0=acc[:], scalar=mod48[:, l, 32 + j:33 + j],
                                            in1=xc_[:, n * 512:(n + 1) * 512], op0=ALU.mult, op1=ALU.add), [acc, mod48, xc_], [xc_])
                                    P.dma(xT_d[j * 128:(j + 1) * 128, cA:cA + NTP], xc_[:], reads=[xc_], writes=[b_xT[j]])

                    if l + 1 < nlayers:
                        with P.scope():
                            hs = P.sb("hs", [128, KD, HALO])
                            P.dma(hs[:], xT_d[:, HALO + NTOK - HALO:HALO + NTOK].rearrange("(k p) n -> p k n", p=128), reads=b_xT, writes=[hs])
                            P.dma(xh_snd.ap().rearrange("(k p) n -> p k n", p=128), hs[:], reads=[hs], writes=[b_xhsnd])
                            allgather(xh_snd, xh_rcv, b_xhsnd, b_xhrcv)
                            hr = P.sb("hr", [128, 4, KD, HALO])
                            for j in range(4):
                                P.dma(hr[:, j], xh_rcv[j * D:(j + 1) * D, :].rearrange("(k p) n -> p k n", p=128), reads=[b_xhrcv], writes=[hr])
                            hsel = P.sb("hsel", [128, KD, HALO])
                            V(lambda e: e.tensor_scalar(out=hsel[:], in0=hr[:, 0], scalar1=flags[:, 4:5], scalar2=None, op0=ALU.mult),
                              [hr, flags], [hsel])
                            for j in range(1, 4):
                                V(lambda e, j=j: e.scalar_tensor_tensor(out=hsel[:], in0=hr[:, j], scalar=flags[:, 4 + j:5 + j], in1=hsel[:],
                                                                        op0=ALU.mult, op1=ALU.add), [hr, flags, hsel], [hsel])
                            P.dma(xT_d[:, 0:HALO].rearrange("(k p) n -> p k n", p=128), hsel[:], reads=[hsel], writes=[b_xTh])

            ck(11)
            with P.scope():
                fg = P.sb("fg", [128, KD])
                P.dma(fg[:], final_g.rearrange("(k p) -> p k", p=128), writes=[fg])
                xe = [P.sb("xe%d" % i, [128, 512]) for i in range(3)]
                sqe = [P.sb("sqe%d" % i, [128, 512]) for i in range(2)]
                xn = P.sb("xn", [128, KD, 512])
                mse = P.sb("mse", [128, 512])
                rse = P.sb("rse", [128, 512])
                orow = [P.sb("orow%d" % i, [128, D]) for i in range(2)]
                for q4 in range(4):
                    cA = HALO + q4 * 512
                    for k in range(KD):
                        P.dma(xn[:, k, :], xT_d[k * 128:(k + 1) * 128, cA:cA + 512], reads=[b_xT[k]], writes=[xn])
                        s_ = sqe[k % 2]
                        A(lambda e, s_=s_, k=k: e.activation(out=s_[:], in_=xn[:, k, :], func=AF.Square), [xn], [s_])
                        MM(PC[:], ones_f[:], s_[:], [ones_f, s_], [R_cv, R_pc3], start=(k == 0), stop=(k == KD - 1), inc=True)
                    V(lambda e: e.tensor_scalar(out=mse[:], in0=PC[:], scalar1=1.0 / D, scalar2=EPS, op0=ALU.mult, op1=ALU.add), [R_cv, R_pc3], [mse])
                    rsqrt(rse[:], mse, mse[:], 512, [rse])
                    for k in range(KD):
                        V(lambda e, k=k: e.scalar_tensor_tensor(out=xn[:, k, :], in0=xn[:, k, :], scalar=fg[:, k:k + 1], in1=rse[:],
                                                                op0=ALU.mult, op1=ALU.mult), [xn, fg, rse], [xn])
                    for t in range(4):
                        ob = orow[t % 2]
                        for k4 in range(4):
                            acc = PA[k4 % 2]
                            for kk in range(4):
                                k = k4 * 4 + kk
                                TR(acc[:, kk * 128:(kk + 1) * 128], xn[:, k, t * 128:(t + 1) * 128], ident_f[:], [xn, ident_f], [acc])
                            if k4 % 2 == 0:
                                V(lambda e, acc=acc, ob=ob, k4=k4: e.tensor_copy(out=ob[:, k4 * 512:(k4 + 1) * 512], in_=acc[:]), [acc], [ob])
                            else:
                                A(lambda e, acc=acc, ob=ob, k4=k4: e.activation(out=ob[:, k4 * 512:(k4 + 1) * 512], in_=acc[:], func=AF.Copy), [acc], [ob])
                        tok0 = q4 * 512 + t * 128
                        P.dma(out_d[tok0:tok0 + 128, :], ob[:], reads=[ob], writes=[b_out])
        except _Stop:
            while len(P.stacks) > 1:
                P.stacks.pop().close()
        with nc.allow_non_contiguous_dma(reason="small strided parameter loads"):
            P.emit()
        nc._prog_stats = (P.nops, {k: v for k, v in P.cnt.items()})
    return nc


def make_in_maps(inputs, wl=DEPTH):
    x = np.ascontiguousarray(np.asarray(inputs["x"], dtype=np.float32))
    cst = host_consts()
    maps = []
    for c in range(NCORE):
        b, s = c // 4, c % 4
        m = {}
        m["x_in"] = np.ascontiguousarray(x[b, s * NTOK:(s + 1) * NTOK, :])
        if s == 0:
            m["xh_in"] = np.zeros((HALO, D), np.float32)
        else:
            m["xh_in"] = np.ascontiguousarray(x[b, s * NTOK - HALO:s * NTOK, :])
        m["c_in"] = np.ascontiguousarray(np.asarray(inputs["c"], np.float32)[b])
        m["w_ada"] = np.ascontiguousarray(np.asarray(inputs["w_ada"], np.float32)[:wl, :, s * 1536:(s + 1) * 1536])
        m["b_ada"] = np.ascontiguousarray(np.asarray(inputs["b_ada"], np.float32)[:wl, s * 1536:(s + 1) * 1536])
        m["w_in"] = np.ascontiguousarray(np.asarray(inputs["w_in"], np.float32)[:wl])
        m["w_out"] = np.ascontiguousarray(np.asarray(inputs["w_out"], np.float32)[:wl])
        for k in ("norm_g", "conv_qkv", "a_log", "dt_bias", "dn_norm_g", "sg_ln_g", "sg_ln_b", "sg_w",
                  "sg_b", "cv_w", "cv_b", "cv_ln_g", "cv_ln_b", "final_g"):
            m[k] = np.ascontiguousarray(np.asarray(inputs[k], np.float32))
        fl = np.zeros((128, 12), np.float32)
        for j in range(4):
            fl[:, j] = 1.0 if j < s else 0.0
            fl[:, 4 + j] = 1.0 if j == s - 1 else 0.0
        fl[:, 8] = 0.0 if s == 0 else 1.0
        m["flags"] = fl
        m["k_ident"] = cst["ident"]
        m["k_ones"] = cst["ones"]
        m["k_maskneg"] = cst["maskneg"]
        m["k_strict01"] = cst["strict01"]
        m["k_triu"] = cst["triu"]
        m["k_lvlmask"] = cst["lvlmask"]
        m["k_wsmask"] = cst["wsmask"]
        maps.append(m)
    return maps


_NC_CACHE = {}


def kernel(**inputs):
    if "nc" not in _NC_CACHE:
        _NC_CACHE["nc"] = build()
    nc = _NC_CACHE["nc"]
    maps = make_in_maps(inputs)
    res = run_bass_kernel_spmd(nc, maps, core_ids=list(range(NCORE)))
    out = np.zeros((2, SEQ, D), np.float32)
    for c in range(NCORE):
        b, s = c // 4, c % 4
        out[b, s * NTOK:(s + 1) * NTOK, :] = res.results[c]["out"]
    return out
```
